# Optimizing a Trainium2 kernel written in Bass

```python
import math
import jax, jax.numpy as jnp
from jax import lax
import numpy as np

D_MODEL = 4096
BATCH = 1
SEQ = 8192
DEPTH = 1

HEAD_DIM = 128
DIFF_V_DIM = 2 * HEAD_DIM
DIFF_HEADS = (D_MODEL // 2) // DIFF_V_DIM
DIFF_WIDTH = DIFF_HEADS * DIFF_V_DIM
NSA_HEADS = (D_MODEL - DIFF_WIDTH) // HEAD_DIM
NSA_KV_HEADS = 4
NSA_GROUP = NSA_HEADS // NSA_KV_HEADS
NSA_WIDTH = NSA_HEADS * HEAD_DIM
CMP_BLOCK = 32
CMP_STRIDE = 16
CMP_HIDDEN = 256
SEL_BLOCK = 64
SEL_TOPN = 16
WINDOW = 512
Q_BLOCK = 128
D_FF = 11008
EPS = 1e-6
NEG = -1e30
FORCE_SCORE = 1e4

SPLIT_SIZES = [
    DIFF_HEADS * 2 * HEAD_DIM,
    DIFF_HEADS * 2 * HEAD_DIM,
    DIFF_HEADS * DIFF_V_DIM,
    NSA_HEADS * HEAD_DIM,
    NSA_KV_HEADS * HEAD_DIM,
    NSA_KV_HEADS * HEAD_DIM,
    NSA_KV_HEADS * HEAD_DIM,
    NSA_KV_HEADS * HEAD_DIM,
    NSA_KV_HEADS * HEAD_DIM,
    NSA_KV_HEADS * HEAD_DIM,
    3 * NSA_HEADS,
]
N_IN = sum(SPLIT_SIZES)

kernel_name = "hybrid_diff_nsa_macaron_alibi"


def rmsnorm(x, g):
    x32 = x.astype(jnp.float32)
    y = x32 * lax.rsqrt(jnp.mean(x32 * x32, axis=-1, keepdims=True) + EPS)
    return (y * g.astype(jnp.float32)).astype(x.dtype)


def swiglu(x, w_gate, w_up, w_down):
    return (jax.nn.silu(x @ w_gate) * (x @ w_up)) @ w_down


def alibi_slopes(n):
    return 2.0 ** (-8.0 * jnp.arange(1, n + 1, dtype=jnp.float32) / n)


def lambda_init(layer):
    return 0.8 - 0.6 * math.exp(-0.3 * layer)


def diff_attention(q, k, v, lam, slopes):
    B, S, H, _, Dh = q.shape
    scale = Dh ** -0.5
    kpos = jnp.arange(S)

    def block(i):
        s0 = i * Q_BLOCK
        qb = lax.dynamic_slice_in_dim(q, s0, Q_BLOCK, axis=1)
        dist = (s0 + jnp.arange(Q_BLOCK))[:, None] - kpos[None, :]
        bias = -slopes[:, None, None, None] * dist
        s = jnp.einsum('bqhmd,bkhmd->bhmqk', qb, k) * scale + bias
        p = jax.nn.softmax(jnp.where(dist >= 0, s, NEG), axis=-1)
        a = p[:, :, 0] - lam * p[:, :, 1]
        return jnp.einsum('bhqk,bkhe->bqhe', a, v)

    o = lax.map(block, jnp.arange(S // Q_BLOCK))
    return jnp.moveaxis(o, 0, 1).reshape(B, S, H, -1)


def compress(kv, pos, w1, w2):
    B, S, Hkv, Dh = kv.shape
    n_cmp = (S - CMP_BLOCK) // CMP_STRIDE + 1
    idx = jnp.arange(n_cmp)[:, None] * CMP_STRIDE + jnp.arange(CMP_BLOCK)[None, :]
    blocks = kv[:, idx] + pos[:, None, :]
    blocks = blocks.transpose(0, 1, 3, 2, 4).reshape(B, n_cmp, Hkv, CMP_BLOCK * Dh)
    return jax.nn.gelu(blocks @ w1) @ w2


def native_sparse_attention(q, k_cmp, v_cmp, k_sel, v_sel, k_win, v_win, gates,
                            pos_k, w1_k, w2_k, pos_v, w1_v, w2_v, slopes):
    B, S, Hkv, G, Dh = q.shape
    scale = Dh ** -0.5
    kc = compress(k_cmp, pos_k, w1_k, w2_k)
    vc = compress(v_cmp, pos_v, w1_v, w2_v)
    n_cmp = kc.shape[1]
    cmp_start = jnp.arange(n_cmp) * CMP_STRIDE
    cmp_end = cmp_start + CMP_BLOCK - 1
    n_sel = S // SEL_BLOCK
    topn = min(SEL_TOPN, n_sel)
    sel_start = jnp.arange(n_sel) * SEL_BLOCK
    overlap = ((cmp_start[:, None] <= sel_start[None, :] + SEL_BLOCK - 1)
               & (cmp_end[:, None] >= sel_start[None, :])).astype(jnp.float32)
    ks_t = k_sel.reshape(B, n_sel, SEL_BLOCK, Hkv, Dh).transpose(0, 3, 1, 2, 4)
    vs_t = v_sel.reshape(B, n_sel, SEL_BLOCK, Hkv, Dh).transpose(0, 3, 1, 2, 4)
    bi = jnp.arange(B)[:, None, None, None]
    hi = jnp.arange(Hkv)[None, :, None, None]
    pad = ((0, 0), (WINDOW, 0), (0, 0), (0, 0))
    kw_pad = jnp.pad(k_win, pad)
    vw_pad = jnp.pad(v_win, pad)
    sl = slopes[:, :, None, None]
    blk_ids = jnp.arange(n_sel)

    def block(i):
        s0 = i * Q_BLOCK
        t = s0 + jnp.arange(Q_BLOCK)
        qb = lax.dynamic_slice_in_dim(q, s0, Q_BLOCK, axis=1)
        gb = lax.dynamic_slice_in_dim(gates, s0, Q_BLOCK, axis=1)
        dist_c = t[:, None] - cmp_end[None, :]
        ok_c = dist_c >= 0
        s_c = jnp.einsum('bqhgd,bchd->bhgqc', qb, kc) * scale - sl * dist_c
        p_c = jax.nn.softmax(jnp.where(ok_c, s_c, NEG), axis=-1) * ok_c
        o_c = jnp.einsum('bhgqc,bchd->bqhgd', p_c, vc)
        imp = jnp.einsum('bhgqc,cn->bhqn', p_c, overlap)
        cur = t // SEL_BLOCK
        valid = blk_ids[None, :] <= cur[:, None]
        forced = ((blk_ids[None, :] == 0) | (blk_ids[None, :] == cur[:, None])
                  | (blk_ids[None, :] == cur[:, None] - 1))
        score = jnp.where(forced, FORCE_SCORE, jnp.where(valid, imp, -1.0))
        _, sel = lax.top_k(score, topn)
        kg = ks_t[bi, hi, sel].reshape(B, Hkv, Q_BLOCK, topn * SEL_BLOCK, Dh)
        vg = vs_t[bi, hi, sel].reshape(B, Hkv, Q_BLOCK, topn * SEL_BLOCK, Dh)
        pos_s = sel[..., None] * SEL_BLOCK + jnp.arange(SEL_BLOCK)
        dist_s = (t[:, None, None] - pos_s).reshape(B, Hkv, 1, Q_BLOCK, topn * SEL_BLOCK)
        s_s = jnp.einsum('bqhgd,bhqmd->bhgqm', qb, kg) * scale - sl * dist_s
        p_s = jax.nn.softmax(jnp.where(dist_s >= 0, s_s, NEG), axis=-1)
        o_s = jnp.einsum('bhgqm,bhqmd->bqhgd', p_s, vg)
        kw = lax.dynamic_slice_in_dim(kw_pad, s0, WINDOW + Q_BLOCK, axis=1)
        vw = lax.dynamic_slice_in_dim(vw_pad, s0, WINDOW + Q_BLOCK, axis=1)
        kpos = s0 - WINDOW + jnp.arange(WINDOW + Q_BLOCK)
        dist_w = t[:, None] - kpos[None, :]
        ok_w = (dist_w >= 0) & (dist_w < WINDOW) & (kpos[None, :] >= 0)
        s_w = jnp.einsum('bqhgd,bkhd->bhgqk', qb, kw) * scale - sl * dist_w
        p_w = jax.nn.softmax(jnp.where(ok_w, s_w, NEG), axis=-1)
        o_w = jnp.einsum('bhgqk,bkhd->bqhgd', p_w, vw)
        return gb[..., 0:1] * o_c + gb[..., 1:2] * o_s + gb[..., 2:3] * o_w

    o = lax.map(block, jnp.arange(S // Q_BLOCK))
    return jnp.moveaxis(o, 0, 1).reshape(B, S, Hkv * G * Dh)


def setup_inputs(seed: int = 0) -> dict:
    key = jax.random.key(seed)
    ks = jax.random.split(key, 32)

    def nrm(k, shape, scale):
        return jax.random.normal(k, shape, jnp.float32) * scale

    def gain(k, shape):
        return 1.0 + 0.05 * jax.random.normal(k, shape, jnp.float32)

    L, D = DEPTH, D_MODEL
    cin = CMP_BLOCK * HEAD_DIM
    return {
        "x": nrm(ks[0], (BATCH, SEQ, D), 1.0),
        "ffn1_norm": gain(ks[1], (L, D)),
        "ffn1_w_gate": nrm(ks[2], (L, D, D_FF), D ** -0.5),
        "ffn1_w_up": nrm(ks[3], (L, D, D_FF), D ** -0.5),
        "ffn1_w_down": nrm(ks[4], (L, D_FF, D), D_FF ** -0.5),
        "mix_norm": gain(ks[5], (L, D)),
        "w_in": nrm(ks[6], (L, D, N_IN), D ** -0.5),
        "gate_bias": nrm(ks[7], (L, 3 * NSA_HEADS), 0.1),
        "lambda_q1": nrm(ks[8], (L, HEAD_DIM), 0.1),
        "lambda_k1": nrm(ks[9], (L, HEAD_DIM), 0.1),
        "lambda_q2": nrm(ks[10], (L, HEAD_DIM), 0.1),
        "lambda_k2": nrm(ks[11], (L, HEAD_DIM), 0.1),
        "diff_norm": gain(ks[12], (L, DIFF_V_DIM)),
        "cmp_pos_k": nrm(ks[13], (L, CMP_BLOCK, HEAD_DIM), 0.1),
        "cmp_w1_k": nrm(ks[14], (L, cin, CMP_HIDDEN), cin ** -0.5),
        "cmp_w2_k": nrm(ks[15], (L, CMP_HIDDEN, HEAD_DIM), CMP_HIDDEN ** -0.5),
        "cmp_pos_v": nrm(ks[16], (L, CMP_BLOCK, HEAD_DIM), 0.1),
        "cmp_w1_v": nrm(ks[17], (L, cin, CMP_HIDDEN), cin ** -0.5),
        "cmp_w2_v": nrm(ks[18], (L, CMP_HIDDEN, HEAD_DIM), CMP_HIDDEN ** -0.5),
        "w_out": nrm(ks[19], (L, D, D), D ** -0.5),
        "ffn2_norm": gain(ks[20], (L, D)),
        "ffn2_w_gate": nrm(ks[21], (L, D, D_FF), D ** -0.5),
        "ffn2_w_up": nrm(ks[22], (L, D, D_FF), D ** -0.5),
        "ffn2_w_down": nrm(ks[23], (L, D_FF, D), D_FF ** -0.5),
        "final_norm": gain(ks[24], (D,)),
    }


def reference(x, ffn1_norm, ffn1_w_gate, ffn1_w_up, ffn1_w_down, mix_norm, w_in, gate_bias,
              lambda_q1, lambda_k1, lambda_q2, lambda_k2, diff_norm,
              cmp_pos_k, cmp_w1_k, cmp_w2_k, cmp_pos_v, cmp_w1_v, cmp_w2_v,
              w_out, ffn2_norm, ffn2_w_gate, ffn2_w_up, ffn2_w_down, final_norm):
    B, S, _ = x.shape
    f32 = jnp.float32
    offsets = [int(o) for o in np.cumsum(SPLIT_SIZES)[:-1]]
    diff_slopes = alibi_slopes(DIFF_HEADS)
    nsa_slopes = alibi_slopes(NSA_HEADS).reshape(NSA_KV_HEADS, NSA_GROUP)
    kv_shape = (B, S, NSA_KV_HEADS, HEAD_DIM)
    h = x
    for l in range(DEPTH):
        h = h + 0.5 * swiglu(rmsnorm(h, ffn1_norm[l]), ffn1_w_gate[l], ffn1_w_up[l], ffn1_w_down[l])
        u = rmsnorm(h, mix_norm[l])
        proj = (u @ w_in[l]).astype(f32)
        dq, dk, dv, nq, kc, vc, ksl, vsl, kwn, vwn, g = jnp.split(proj, offsets, axis=-1)
        lam0 = lambda_init(l)
        lam = (jnp.exp(jnp.dot(lambda_q1[l].astype(f32), lambda_k1[l].astype(f32)))
               - jnp.exp(jnp.dot(lambda_q2[l].astype(f32), lambda_k2[l].astype(f32))) + lam0)
        o_diff = diff_attention(dq.reshape(B, S, DIFF_HEADS, 2, HEAD_DIM),
                                dk.reshape(B, S, DIFF_HEADS, 2, HEAD_DIM),
                                dv.reshape(B, S, DIFF_HEADS, DIFF_V_DIM), lam, diff_slopes)
        o_diff = rmsnorm(o_diff, diff_norm[l]) * (1.0 - lam0)
        gates = jax.nn.sigmoid(g + gate_bias[l].astype(f32)).reshape(B, S, NSA_KV_HEADS, NSA_GROUP, 3)
        o_nsa = native_sparse_attention(
            nq.reshape(B, S, NSA_KV_HEADS, NSA_GROUP, HEAD_DIM),
            kc.reshape(kv_shape), vc.reshape(kv_shape),
            ksl.reshape(kv_shape), vsl.reshape(kv_shape),
            kwn.reshape(kv_shape), vwn.reshape(kv_shape), gates,
            cmp_pos_k[l].astype(f32), cmp_w1_k[l].astype(f32), cmp_w2_k[l].astype(f32),
            cmp_pos_v[l].astype(f32), cmp_w1_v[l].astype(f32), cmp_w2_v[l].astype(f32),
            nsa_slopes)
        mix = jnp.concatenate([o_diff.reshape(B, S, DIFF_WIDTH), o_nsa], axis=-1).astype(h.dtype)
        h = h + mix @ w_out[l]
        h = h + 0.5 * swiglu(rmsnorm(h, ffn2_norm[l]), ffn2_w_gate[l], ffn2_w_up[l], ffn2_w_down[l])
    return rmsnorm(h, final_norm)
```

```python
import contextlib
import numpy as np
import ml_dtypes
import concourse.bass as bass
import concourse.mybir as mybir
from concourse.bass_utils import run_bass_kernel_spmd

F32 = mybir.dt.float32
BF16 = mybir.dt.bfloat16
AF = mybir.ActivationFunctionType
ALU = mybir.AluOpType
AX = mybir.AxisListType

NCORES = 8
D = 4096
S = 8192
DFF = 11008
NFC = DFF // 128
TOK = S // NCORES
TP = 512
KT = D // 128
EPS = 1e-6
N_IN = 11312


class Tok:
    __slots__ = ("w", "r", "name")

    def __init__(self, name=""):
        self.w = None
        self.r = []
        self.name = name


class Eng:
    def __init__(self, key, e, sem):
        self.key = key
        self.e = e
        self.sem = sem
        self.n = 0
        self.seen = {}
        self.pending = False


class DSem:
    def __init__(self, key, sem):
        self.key = key
        self.sem = sem
        self.n = 0


class K:
    SAME_ENG_SYNC = True

    def __init__(self, nc, es):
        self.nc = nc
        self.es = es
        self.sems = {}
        self.engs = {}
        for key in ("pe", "dve", "act", "pool", "sp"):
            sem = es.enter_context(nc.semaphore("tl_" + key))
            e = {"pe": nc.tensor, "dve": nc.vector, "act": nc.scalar, "pool": nc.gpsimd, "sp": nc.sync}[key]
            self.engs[key] = Eng(key, e, sem)
            self.sems[key] = sem
        self.ndsem = 0
        self.dsl = []

    def dsem(self):
        key = "d%d" % self.ndsem
        self.ndsem += 1
        sem = self.es.enter_context(self.nc.semaphore(key))
        self.sems[key] = sem
        ds = DSem(key, sem)
        self.dsl.append(ds)
        return ds

    def _wait(self, eng, deps):
        best = {}
        for d in deps:
            if d is None:
                continue
            k, v = d
            if best.get(k, 0) < v:
                best[k] = v
        for k, v in best.items():
            if k == eng.key and not (self.SAME_ENG_SYNC and k != "pe"):
                continue
            if eng.seen.get(k, 0) >= v:
                continue
            eng.e.wait_ge(self.sems[k], v)
            eng.seen[k] = v

    @staticmethod
    def _deps(reads, writes):
        deps = []
        for t in reads:
            deps.append(t.w)
        for t in writes:
            deps.append(t.w)
            deps.extend(t.r)
        return deps

    def op(self, ekey, fn, reads=(), writes=(), inc=True):
        eng = self.engs[ekey]
        self._wait(eng, self._deps(reads, writes))
        ins = fn(eng.e)
        if inc:
            eng.n += 1
            ins.then_inc(eng.sem, 1)
            me = (eng.key, eng.n)
            eng.pending = False
        else:
            me = (eng.key, eng.n + 1)
            eng.pending = True
        for t in reads:
            t.r.append(me)
        for t in writes:
            t.w = me
            t.r = []
        return ins

    def dma(self, qkey, ds, out, in_, reads=(), writes=(), **kw):
        eng = self.engs[qkey]
        deps = self._deps(reads, writes)
        if ds.n:
            deps.append((ds.key, ds.n))
        self._wait(eng, deps)
        ds.n += 16
        eng.e.dma_start(out=out, in_=in_, **kw).then_inc(ds.sem, 16)
        me = (ds.key, ds.n)
        for t in reads:
            t.r.append(me)
        for t in writes:
            t.w = me
            t.r = []

    def dma_group(self, qkey, ds, pairs, reads=(), writes=()):
        eng = self.engs[qkey]
        deps = self._deps(reads, writes)
        if ds.n:
            deps.append((ds.key, ds.n))
        self._wait(eng, deps)
        for out, in_ in pairs:
            ds.n += 16
            eng.e.dma_start(out=out, in_=in_).then_inc(ds.sem, 16)
        me = (ds.key, ds.n)
        for t in reads:
            t.r.append(me)
        for t in writes:
            t.w = me
            t.r = []

    def wait_all(self, ekey, toks):
        eng = self.engs[ekey]
        deps = []
        for t in toks:
            deps.append(t.w)
            deps.extend(t.r)
        self._wait(eng, deps)


class Ring:
    def __init__(self, k, aps, with_sem=False, name="ring"):
        self.aps = aps
        self.toks = [Tok("%s%d" % (name, i)) for i in range(len(aps))]
        self.ds = [k.dsem() for _ in aps] if with_sem else None
        self.i = 0

    def next(self):
        j = self.i % len(self.aps)
        self.i += 1
        return j


class WStream:
    NST = 4
    NBF = 8
    CAST_ENGS = ("act", "pool", "dve")

    def __init__(self, k, nc, es, tag):
        self.k = k
        st = [es.enter_context(nc.sbuf_tensor("%s_st%d" % (tag, i), [128, 2048], F32)) for i in range(self.NST)]
        bf = [es.enter_context(nc.sbuf_tensor("%s_bf%d" % (tag, i), [128, 2048], BF16)) for i in range(self.NBF)]
        self.stage = Ring(k, st, with_sem=True, name=tag + "st")
        self.bf = Ring(k, bf, name=tag + "bf")
        self.queue = []
        self.ncast = 0

    def issue(self, src_ap, shape3=None):
        k = self.k
        j = self.stage.next()
        dst = self.stage.aps[j]
        if shape3 is not None:
            dv = dst[:, 0:shape3[0] * shape3[1]].rearrange("p (a b) -> p a b", a=shape3[0])
        else:
            dv = dst[:, :]
        k.dma("sp", self.stage.ds[j], dv, src_ap, writes=[self.stage.toks[j]])
        self.queue.append(j)

    def cast(self):
        k = self.k
        j = self.queue.pop(0)
        b = self.bf.next()
        ek = self.CAST_ENGS[self.ncast % len(self.CAST_ENGS)]
        self.ncast += 1
        src = self.stage.aps[j]
        dst = self.bf.aps[b]
        if ek == "act":
            fn = lambda e: e.copy(out=dst[:, :], in_=src[:, :])
        else:
            fn = lambda e: e.tensor_copy(out=dst[:, :], in_=src[:, :])
        k.op(ek, fn, reads=[self.stage.toks[j]], writes=[self.bf.toks[b]])
        return dst, self.bf.toks[b]


def stream_pieces(ws, pieces, consume, lookahead=3):
    n = len(pieces)
    issued = 0
    casted = []
    assert len(ws.stage.aps) >= lookahead and len(ws.bf.aps) >= 2
    for i in range(n):
        while issued < min(n, i + lookahead):
            ws.issue(pieces[issued][0], pieces[issued][1])
            issued += 1
        while len(casted) < min(n, i + 2):
            casted.append(ws.cast())
        consume(i, pieces[i][2], casted[i][0], casted[i][1])


def rms_transpose(k, nc, C, h_tiles, h_toks, gainT, xT, xT_tok, ntt):
    ybuf, ytok = C["ybuf"], C["ytok"]
    ss, sstok = C["ss"], C["sstok"]
    ident = C["ident"]
    pst = C["ps_t"]
    xTv = xT[:, :].rearrange("p (a b) -> p a b", a=KT)
    for tt in range(ntt):
        ht = h_tiles[tt]
        k.op("act", lambda e: e.activation(out=ybuf[:, :], in_=ht[:, :], func=AF.Square, accum_out=ss[:, 0:1]),
             reads=[h_toks[tt]], writes=[ytok, sstok])
        k.op("dve", lambda e: e.tensor_scalar(out=ss[:, 1:2], in0=ss[:, 0:1], scalar1=1.0 / D, scalar2=EPS,
                                              op0=ALU.mult, op1=ALU.add), reads=[sstok], writes=[sstok])
        k.op("act", lambda e: e.activation(out=ss[:, 3:4], in_=ss[:, 1:2], func=AF.Sqrt), reads=[sstok], writes=[sstok])
        k.op("dve", lambda e: e.reciprocal(out=ss[:, 2:3], in_=ss[:, 3:4]), reads=[sstok], writes=[sstok])
        k.op("dve", lambda e: e.tensor_scalar(out=ybuf[:, :], in0=ht[:, :], scalar1=ss[:, 2:3], scalar2=None,
                                              op0=ALU.mult), reads=[h_toks[tt], sstok], writes=[ytok])
        for kq in range(KT // 4):
            b = pst.next()
            pt = pst.aps[b]
            for j in range(4):
                kt = kq * 4 + j
                k.op("pe", lambda e: e.transpose(out=pt[:, j * 128:(j + 1) * 128],
                                                 in_=ybuf[:, kt * 128:(kt + 1) * 128], identity=ident[:, :]),
                     reads=[ytok], writes=[pst.toks[b]], inc=(j == 3))
            for j in range(4):
                kt = kq * 4 + j
                ek = "act" if j % 2 == 0 else "dve"
                if ek == "act":
                    fn = lambda e: e.activation(out=xTv[:, kt, tt * 128:(tt + 1) * 128],
                                                in_=pt[:, j * 128:(j + 1) * 128], func=AF.Copy,
                                                scale=gainT[:, kt:kt + 1])
                else:
                    fn = lambda e: e.tensor_scalar(out=xTv[:, kt, tt * 128:(tt + 1) * 128],
                                                   in0=pt[:, j * 128:(j + 1) * 128],
                                                   scalar1=gainT[:, kt:kt + 1], scalar2=None, op0=ALU.mult)
                k.op(ek, fn, reads=[pst.toks[b]], writes=[xT_tok])


def ffn_pass(k, nc, C, ws, h_in, h_in_tok, h_out, h_out_tok, gainT, wg, wu, wd, row0, final_gain=None):
    hacc, hacc_tok = C["hacc"], C["hacc_tok"]
    xT, xT_tok = C["xT"], C["xT_tok"]
    ld = C["ld_sems"]
    for tt in range(4):
        k.dma("sp", ld[tt], hacc[tt][:, :], h_in[row0 + tt * 128: row0 + (tt + 1) * 128, :],
              reads=[h_in_tok], writes=[hacc_tok[tt]])
    rms_transpose(k, nc, C, hacc, hacc_tok, gainT, xT, xT_tok, 4)
    xTv = xT[:, :].rearrange("p (a b) -> p a b", a=KT)

    groups = [list(range(g, min(g + 2, NFC))) for g in range(0, NFC, 2)]
    pieces = []

    def gu_pieces(fc):
        for m, w in (("g", wg), ("u", wu)):
            for kh in range(2):
                src = w[kh * 2048:(kh + 1) * 2048, fc * 128:(fc + 1) * 128].rearrange("(kt p) f -> p kt f", p=128)
                pieces.append((src, (16, 128), (m, fc, kh)))

    def d_pieces(grp):
        for dh in range(2):
            for fc in grp:
                src = wd[fc * 128:(fc + 1) * 128, dh * 2048:(dh + 1) * 2048]
                pieces.append((src, None, ("d", fc, dh, grp)))

    for gi, grp in enumerate(groups):
        for fc in grp:
            gu_pieces(fc)
        if gi >= 1:
            d_pieces(groups[gi - 1])
    d_pieces(groups[-1])

    psgu, psd = C["ps_gu"], C["ps_d"]
    act_ring = C["act_ring"]
    sg_ring = C["sg_ring"]
    st = {"gu_bank": {}, "act_slot": {}, "dbuf": {}}

    def consume(i, pay, wb, wtok):
        if pay[0] in ("g", "u"):
            m, fc, kh = pay
            if kh == 0:
                b = psgu.next()
                st["gu_bank"][(m, fc)] = b
            b = st["gu_bank"][(m, fc)]
            wv = wb[:, :].rearrange("p (a b) -> p a b", a=16)
            for j in range(16):
                kt = kh * 16 + j
                k.op("pe", lambda e: e.matmul(psgu.aps[b][:, :], lhsT=wv[:, j, :], rhs=xTv[:, kt, :],
                                              start=(kt == 0), stop=(kt == KT - 1)),
                     reads=[wtok, xT_tok], writes=[psgu.toks[b]], inc=(j == 15))
            if m == "u" and kh == 1:
                bg = st["gu_bank"][("g", fc)]
                bu = b
                sj = sg_ring.next()
                sgt = sg_ring.aps[sj]
                k.op("act", lambda e: e.activation(out=sgt[:, :], in_=psgu.aps[bg][:, :], func=AF.Silu),
                     reads=[psgu.toks[bg]], writes=[sg_ring.toks[sj]])
                aj = act_ring.next()
                st["act_slot"][fc] = aj
                k.op("dve", lambda e: e.tensor_tensor(out=act_ring.aps[aj][:, :], in0=sgt[:, :],
                                                      in1=psgu.aps[bu][:, :], op=ALU.mult),
                     reads=[sg_ring.toks[sj], psgu.toks[bu]], writes=[act_ring.toks[aj]])
        else:
            _, fc, dh, grp = pay
            st["dbuf"][(fc, dh)] = (wb, wtok)
            if fc != grp[-1]:
                return
            for tt in range(4):
                for d4 in range(4):
                    b = psd.next()
                    for gi2, f2 in enumerate(grp):
                        wb2, wtok2 = st["dbuf"][(f2, dh)]
                        aj = st["act_slot"][f2]
                        k.op("pe", lambda e: e.matmul(psd.aps[b][:, :],
                                                      lhsT=act_ring.aps[aj][:, tt * 128:(tt + 1) * 128],
                                                      rhs=wb2[:, d4 * 512:(d4 + 1) * 512],
                                                      start=(gi2 == 0), stop=(gi2 == len(grp) - 1)),
                             reads=[wtok2, act_ring.toks[aj]], writes=[psd.toks[b]],
                             inc=(gi2 == len(grp) - 1))
                    c0 = dh * 2048 + d4 * 512
                    k.op("dve", lambda e: e.scalar_tensor_tensor(out=hacc[tt][:, c0:c0 + 512],
                                                                 in0=psd.aps[b][:, :], scalar=0.5,
                                                                 in1=hacc[tt][:, c0:c0 + 512],
                                                                 op0=ALU.mult, op1=ALU.add),
                         reads=[psd.toks[b], hacc_tok[tt]], writes=[hacc_tok[tt]])

    stream_pieces(ws, pieces, consume)

    if final_gain is None:
        for tt in range(4):
            k.dma("sp", ld[tt], h_out[row0 + tt * 128: row0 + (tt + 1) * 128, :], hacc[tt][:, :],
                  reads=[hacc_tok[tt]], writes=[h_out_tok])
    else:
        ybuf, ytok = C["ybuf"], C["ytok"]
        ss, sstok = C["ss"], C["sstok"]
        for tt in range(4):
            ht = hacc[tt]
            k.op("act", lambda e: e.activation(out=ybuf[:, :], in_=ht[:, :], func=AF.Square, accum_out=ss[:, 0:1]),
                 reads=[hacc_tok[tt]], writes=[ytok, sstok])
            k.op("dve", lambda e: e.tensor_scalar(out=ss[:, 1:2], in0=ss[:, 0:1], scalar1=1.0 / D, scalar2=EPS,
                                                  op0=ALU.mult, op1=ALU.add), reads=[sstok], writes=[sstok])
            k.op("act", lambda e: e.activation(out=ss[:, 3:4], in_=ss[:, 1:2], func=AF.Sqrt), reads=[sstok], writes=[sstok])
            k.op("dve", lambda e: e.reciprocal(out=ss[:, 2:3], in_=ss[:, 3:4]), reads=[sstok], writes=[sstok])
            k.op("dve", lambda e: e.scalar_tensor_tensor(out=ht[:, :], in0=ht[:, :], scalar=ss[:, 2:3],
                                                         in1=final_gain[:, :], op0=ALU.mult, op1=ALU.mult),
                 reads=[hacc_tok[tt], sstok], writes=[hacc_tok[tt]])
            k.dma("sp", ld[tt], h_out[row0 + tt * 128: row0 + (tt + 1) * 128, :], ht[:, :],
                  reads=[hacc_tok[tt]], writes=[h_out_tok])


SFX = [0]


def alloc_common(k, nc, es):
    C = {}
    SFX[0] += 1
    sfx = "_s%d" % SFX[0]
    C["hacc"] = [es.enter_context(nc.sbuf_tensor("hacc%d" % i + sfx, [128, D], F32)) for i in range(4)]
    C["hacc_tok"] = [Tok("hacc%d" % i) for i in range(4)]
    C["xT"] = es.enter_context(nc.sbuf_tensor("xT" + sfx, [128, KT * TP], BF16))
    C["xT_tok"] = Tok("xT")
    C["ybuf"] = es.enter_context(nc.sbuf_tensor("ybuf" + sfx, [128, D], BF16))
    C["ytok"] = Tok("ybuf")
    C["ss"] = es.enter_context(nc.sbuf_tensor("ss" + sfx, [128, 4], F32))
    C["sstok"] = Tok("ss")
    C["ld_sems"] = [k.dsem() for _ in range(4)]
    return C


BIG = 30000.0
SCALE = 128.0 ** -0.5
NDD = 68
OFF_DQ, OFF_DK, OFF_DV, OFF_NQ, OFF_KC, OFF_VC, OFF_KS, OFF_VS, OFF_KW, OFF_VW, OFF_G = (
    0, 2048, 4096, 6144, 8192, 8704, 9216, 9728, 10240, 10752, 11264)


def pos_chunk_loc(pc):
    return (pc, 0) if pc < 8 else (15 - pc, 1)


def barrier(k):
    items = [(e.key, e.n) for e in k.engs.values() if e.n] + [(d.key, d.n) for d in k.dsl if d.n]
    for e in k.engs.values():
        k._wait(e, items)


def alloc_dense(k, nc, es, banks):
    C = alloc_common(k, nc, es)
    sfx = "_s%d" % SFX[0]
    C["ws"] = WStream(k, nc, es, "w" + sfx)
    act_t = [es.enter_context(nc.sbuf_tensor("actT%d" % i + sfx, [128, TP], BF16)) for i in range(4)]
    sg_t = [es.enter_context(nc.sbuf_tensor("sg%d" % i + sfx, [128, TP], F32)) for i in range(2)]
    C["act_ring"] = Ring(k, act_t, name="act")
    C["sg_ring"] = Ring(k, sg_t, name="sg")
    C["ps_gu"] = Ring(k, banks[0:4], name="psgu")
    C["ps_d"] = Ring(k, banks[4:8], name="psd")
    C["ps_t"] = Ring(k, [b.bitcast(BF16) for b in banks[0:4]], name="pst")
    C["ps_t"].toks = C["ps_gu"].toks
    ev = [es.enter_context(nc.sbuf_tensor("ev%d" % i + sfx, [128, TP], BF16)) for i in range(4)]
    C["ev_ring"] = Ring(k, ev, with_sem=True, name="ev")
    C["gt"] = es.enter_context(nc.sbuf_tensor("gt" + sfx, [128, 4 * 48], F32))
    C["gt_tok"] = Tok("gt")
    C["gt_sem"] = k.dsem()
    return C


def tm_pieces(w, col0, ncols, payload):
    pcs = []
    if ncols == 512:
        for kq in range(8):
            src = w[kq * 512:(kq + 1) * 512, col0:col0 + 512].rearrange("(kt p) f -> p kt f", p=128)
            pcs.append((src, (4, 512), (payload, kq, 4, ncols)))
    else:
        src = w[:, col0:col0 + ncols].rearrange("(kt p) f -> p kt f", p=128)
        pcs.append((src, (32, ncols), (payload, 0, 32, ncols)))
    return pcs


def tm_consume(k, C, xTv, xT_tok, pay, wb, wtok, banks_ring, st):
    payload, kq, nkt, ncols = pay
    wv = wb[:, 0:nkt * ncols].rearrange("p (a b) -> p a b", a=nkt)
    if kq == 0:
        st["tmb"] = [banks_ring.next() for _ in range(4)]
    bs = st["tmb"]
    for j in range(nkt):
        kt = kq * nkt + j
        for tt in range(4):
            b = bs[tt]
            k.op("pe", lambda e: e.matmul(banks_ring.aps[b][:, 0:ncols], lhsT=xTv[:, kt, tt * 128:(tt + 1) * 128],
                                          rhs=wv[:, j, :], start=(kt == 0), stop=(kt == KT - 1)),
                 reads=[wtok, xT_tok], writes=[banks_ring.toks[b]], inc=(kt == KT - 1 or (j == nkt - 1 and tt == 3)))
    return kt == KT - 1


def proj_pass(k, nc, C, h1, h1_tok, gainT, w_in, row0, T):
    hacc, hacc_tok = C["hacc"], C["hacc_tok"]
    xT, xT_tok = C["xT"], C["xT_tok"]
    ld = C["ld_sems"]
    ws = C["ws"]
    for tt in range(4):
        k.dma("sp", ld[tt], hacc[tt][:, :], h1[row0 + tt * 128: row0 + (tt + 1) * 128, :],
              reads=[h1_tok], writes=[hacc_tok[tt]])
    rms_transpose(k, nc, C, hacc, hacc_tok, gainT, xT, xT_tok, 4)
    xTv = xT[:, :].rearrange("p (a b) -> p a b", a=KT)
    psgu, psd = C["ps_gu"], C["ps_d"]
    ev = C["ev_ring"]

    fm = []
    for i in range(16):
        fm.append((OFF_DQ + i * 128, "qT", i))
        fm.append((OFF_DK + i * 128, "kT", i))
        fm.append((OFF_NQ + i * 128, "qT", 16 + i))
    for i in range(4):
        fm.append((OFF_KC + i * 128, "kT", 16 + i))
        fm.append((OFF_VC + i * 128, "kT", 20 + i))
        fm.append((OFF_KS + i * 128, "kT", 24 + i))
        fm.append((OFF_KW + i * 128, "kT", 28 + i))
    pieces = []
    for (c0, dn, di) in fm:
        for kh in range(2):
            src = w_in[kh * 2048:(kh + 1) * 2048, c0:c0 + 128].rearrange("(kt p) f -> p kt f", p=128)
            pieces.append((src, (16, 128), ("fm", dn, di, kh)))
    tml = [(OFF_DV + i * 512, 512, ("v", i * 512)) for i in range(4)]
    tml.append((OFF_VS, 512, ("v", 2048)))
    tml.append((OFF_VW, 512, ("v", 2560)))
    tml.append((OFF_G, 48, ("g", 0)))
    for (c0, ncol, pl) in tml:
        pieces.extend(tm_pieces(w_in, c0, ncol, pl))
    st = {}
    nev = [0]

    def evac(dst_ap, src_ap, rtoks, wtoks):
        ek = "act" if nev[0] % 2 == 0 else "dve"
        nev[0] += 1
        if ek == "act":
            k.op("act", lambda e: e.copy(out=dst_ap, in_=src_ap), reads=rtoks, writes=wtoks)
        else:
            k.op("dve", lambda e: e.tensor_copy(out=dst_ap, in_=src_ap), reads=rtoks, writes=wtoks)

    def consume(i, pay, wb, wtok):
        if pay[0] == "fm":
            _, dn, di, kh = pay
            if kh == 0:
                st["b"] = psgu.next()
            b = st["b"]
            wv = wb[:, :].rearrange("p (a b) -> p a b", a=16)
            for j in range(16):
                kt = kh * 16 + j
                k.op("pe", lambda e: e.matmul(psgu.aps[b][:, :], lhsT=wv[:, j, :], rhs=xTv[:, kt, :],
                                              start=(kt == 0), stop=(kt == KT - 1)),
                     reads=[wtok, xT_tok], writes=[psgu.toks[b]], inc=(j == 15))
            if kh == 1:
                j = ev.next()
                evac(ev.aps[j][:, :], psgu.aps[b][:, :], [psgu.toks[b]], [ev.toks[j]])
                dt_, dtok = T[dn]
                k.dma("sp", ev.ds[j], dt_[di * 128:(di + 1) * 128, row0:row0 + TP], ev.aps[j][:, :],
                      reads=[ev.toks[j]], writes=[dtok])
        else:
            done = tm_consume(k, C, xTv, xT_tok, pay, wb, wtok, psd, st)
            if not done:
                return
            kind, dc0 = pay[0]
            for tt in range(4):
                b = st["tmb"][tt]
                if kind == "v":
                    j = ev.next()
                    evac(ev.aps[j][:, :], psd.aps[b][:, :], [psd.toks[b]], [ev.toks[j]])
                    dt_, dtok = T["vC"]
                    k.dma("sp", ev.ds[j], dt_[row0 + tt * 128: row0 + (tt + 1) * 128, dc0:dc0 + 512],
                          ev.aps[j][:, :], reads=[ev.toks[j]], writes=[dtok])
                else:
                    gt, gtt = C["gt"], C["gt_tok"]
                    k.op("dve", lambda e: e.tensor_tensor(out=gt[:, tt * 48:(tt + 1) * 48], in0=psd.aps[b][:, 0:48],
                                                          in1=C["gbias"][:, :], op=ALU.add),
                         reads=[psd.toks[b]], writes=[gtt])
                    k.op("act", lambda e: e.activation(out=gt[:, tt * 48:(tt + 1) * 48],
                                                       in_=gt[:, tt * 48:(tt + 1) * 48], func=AF.Sigmoid),
                         reads=[gtt], writes=[gtt])
            if kind == "g":
                dt_, dtok = T["gates"]
                k.dma("sp", C["gt_sem"], dt_[row0:row0 + TP, :].rearrange("(t p) c -> p t c", p=128),
                      C["gt"][:, :].rearrange("p (t c) -> p t c", t=4), reads=[C["gt_tok"]], writes=[dtok])

    stream_pieces(ws, pieces, consume)


def wout_pass(k, nc, C, h1, h1_tok, mix, mix_tok, w_out, h2, h2_tok, row0):
    hacc, hacc_tok = C["hacc"], C["hacc_tok"]
    xT, xT_tok = C["xT"], C["xT_tok"]
    ld = C["ld_sems"]
    ws = C["ws"]
    ybuf, ytok = C["ybuf"], C["ytok"]
    pst = C["ps_t"]
    ident = C["ident"]
    xTv = xT[:, :].rearrange("p (a b) -> p a b", a=KT)
    for tt in range(4):
        k.dma("sp", ld[tt], hacc[tt][:, :], h1[row0 + tt * 128: row0 + (tt + 1) * 128, :],
              reads=[h1_tok], writes=[hacc_tok[tt]])
    msem = C["gt_sem"]
    for tt in range(4):
        k.dma("sp", msem, ybuf[:, :], mix[row0 + tt * 128: row0 + (tt + 1) * 128, :], reads=[mix_tok], writes=[ytok])
        for kq in range(KT // 4):
            b = pst.next()
            pt = pst.aps[b]
            for j in range(4):
                kt = kq * 4 + j
                k.op("pe", lambda e: e.transpose(out=pt[:, j * 128:(j + 1) * 128],
                                                 in_=ybuf[:, kt * 128:(kt + 1) * 128], identity=ident[:, :]),
                     reads=[ytok], writes=[pst.toks[b]], inc=(j == 3))
            ek = "act" if kq % 2 == 0 else "dve"
            dst = xTv[:, kq * 4:(kq + 1) * 4, tt * 128:(tt + 1) * 128]
            srcv = pt[:, 0:512].rearrange("p (a b) -> p a b", a=4)
            if ek == "act":
                k.op("act", lambda e: e.copy(out=dst, in_=srcv), reads=[pst.toks[b]], writes=[xT_tok])
            else:
                k.op("dve", lambda e: e.tensor_copy(out=dst, in_=srcv), reads=[pst.toks[b]], writes=[xT_tok])
    pieces = []
    for d8 in range(8):
        pieces.extend(tm_pieces(w_out, d8 * 512, 512, ("o", d8 * 512)))
    st = {}
    psd = C["ps_d"]

    def consume(i, pay, wb, wtok):
        done = tm_consume(k, C, xTv, xT_tok, pay, wb, wtok, psd, st)
        if not done:
            return
        c0 = pay[0][1]
        for tt in range(4):
            b = st["tmb"][tt]
            k.op("dve", lambda e: e.tensor_tensor(out=hacc[tt][:, c0:c0 + 512], in0=psd.aps[b][:, :],
                                                  in1=hacc[tt][:, c0:c0 + 512], op=ALU.add),
                 reads=[psd.toks[b], hacc_tok[tt]], writes=[hacc_tok[tt]])

    stream_pieces(ws, pieces, consume)
    for tt in range(4):
        k.dma("sp", ld[tt], h2[row0 + tt * 128: row0 + (tt + 1) * 128, :], hacc[tt][:, :],
              reads=[hacc_tok[tt]], writes=[h2_tok])
PAST = {0: list(range(0, 7)), 1: list(range(0, 15))}
WPAST = {0: list(range(0, 7)), 1: list(range(7, 15))}
NPB = 7 * 16 + 15 * 16
NPW = 7 * 16 + 8 * 16


def pb_col(slot, j, t, s):
    return (0 if slot == 0 else 112) + j * 16 + t * 4 + s


def attention_phase(k, nc, banks, T, X, DBG=None):
    with contextlib.ExitStack() as es:
        def sb(name, shape, dt):
            return es.enter_context(nc.sbuf_tensor(name, shape, dt))
        cs = k.dsem()
        ctok = Tok("aconst")
        ident = sb("a_ident", [128, 128], BF16)
        maskc = sb("a_maskc", [128, 8 * 512], BF16)
        maskcmp = sb("a_maskcmp", [128, 2 * 4 * 512], BF16)
        albcmp = sb("a_albcmp", [128, 2 * 16 * 16], F32)
        overlap = sb("a_overlap", [128, 4 * 128], BF16)
        esel = sb("a_esel", [128, 60 * 128], BF16)
        ediag = sb("a_ediag", [128, 2 * 4 * 128], BF16)
        tkc = sb("a_tkc", [128, 8 * 3 * 128], BF16)
        posb = sb("a_posb", [128, NPB], F32)
        posbw = sb("a_posbw", [128, NPW], F32)
        diagb = sb("a_diagb", [128, 16], F32)
        slopes = sb("a_slopes", [128, 24], F32)
        gates = sb("a_gates", [128, 8 * 48], F32)
        lamv = sb("a_lamv", [128, 512], F32)
        dnorm = sb("a_dnorm", [128, 256], F32)
        w2k = sb("a_w2k", [128, 256], F32)
        w2v = sb("a_w2v", [128, 256], F32)
        posT = sb("a_posT", [128, 64], F32)
        pairs = [(ident[:, :], X["ident"][:, :]), (maskc[:, :], X["maskc"][:, :]), (maskcmp[:, :], X["maskcmp"][:, :]),
                 (albcmp[:, :], X["albcmp"][:, :]), (overlap[:, :], X["overlap"][:, :]), (esel[:, :], X["esel"][:, :]),
                 (ediag[:, :], X["ediag"][:, :]), (tkc[:, :], X["tkc"][:, :]), (posb[:, :], X["posb"][:, :]),
                 (posbw[:, :], X["posbw"][:, :]), (diagb[:, :], X["diagb"][:, :]), (slopes[:, :], X["slopes"][:, :]),
                 (lamv[:, :], X["lam4"][0:1, :].broadcast_to([128, 512])),
                 (dnorm[:, :], X["diff_norm"][0:1, :].broadcast_to([128, 256])),
                 (w2k[:, :].rearrange("p (a b) -> p a b", a=2), X["cmp_w2_k"].rearrange("(a p) d -> p a d", p=128)),
                 (w2v[:, :].rearrange("p (a b) -> p a b", a=2), X["cmp_w2_v"].rearrange("(a p) d -> p a d", p=128)),
                 (posT[:, 0:32], X["cmp_posT_k"][:, :]), (posT[:, 32:64], X["cmp_posT_v"][:, :])]
        gd, gdtok = T["gates"]
        pairs.append((gates[:, :].rearrange("p (t c) -> p t c", t=8), gd[:, :].rearrange("(t p) c -> p t c", p=128)))
        k.dma_group("sp", cs, pairs, reads=[gdtok], writes=[ctok])
        for e in ("pe", "act", "dve", "pool"):
            k.wait_all(e, [ctok])

        sc = sb("a_sc", [128, 16], F32)
        sctok = Tok("sc")
        junk = sb("a_junk", [128, 512], F32)
        jtok = Tok("junk")
        for i in range(2):
            k.op("dve", lambda e: e.tensor_tensor(out=junk[:, 0:128], in0=lamv[:, i * 256:i * 256 + 128],
                                                  in1=lamv[:, i * 256 + 128:i * 256 + 256], op=ALU.mult), writes=[jtok])
            k.op("dve", lambda e: e.reduce_sum(out=sc[:, i:i + 1], in_=junk[:, 0:128], axis=AX.X),
                 reads=[jtok], writes=[sctok])
        k.op("act", lambda e: e.activation(out=sc[:, 2:4], in_=sc[:, 0:2], func=AF.Exp), reads=[sctok], writes=[sctok])
        k.op("dve", lambda e: e.tensor_tensor(out=sc[:, 4:5], in0=sc[:, 2:3], in1=sc[:, 3:4], op=ALU.subtract),
             reads=[sctok], writes=[sctok])
        LAM0 = 0.8 - 0.6 * 1.0
        k.op("dve", lambda e: e.tensor_scalar(out=sc[:, 5:6], in0=sc[:, 4:5], scalar1=LAM0, scalar2=-1.0,
                                              op0=ALU.add, op1=ALU.mult), reads=[sctok], writes=[sctok])
        k.op("dve", lambda e: e.tensor_scalar(out=dnorm[:, :], in0=dnorm[:, :], scalar1=1.0 - LAM0, scalar2=None,
                                              op0=ALU.mult), writes=[ctok])

        KT0 = sb("a_KT0", [128, 15 * 512], BF16)
        KT1 = sb("a_KT1", [128, 8192], BF16)
        KD = sb("a_KD", [128, 2 * 1024], BF16)
        VA = sb("a_VA", [128, 60 * 257], BF16)
        VD = sb("a_VD", [128, 8 * 257], BF16)
        QT = sb("a_QT", [128, 4 * 1024], BF16)
        ktok = [Tok("KT0"), Tok("KT1"), Tok("KD"), Tok("VA"), Tok("VD"), Tok("QT")]
        ksem = [k.dsem() for _ in range(6)]
        PT = [sb("a_PT%d" % i, [128, 512], BF16) for i in range(3)]
        ptr = Ring(k, PT, name="PT")
        bias_h = sb("a_biash", [128, NPB], F32)
        bias_w = sb("a_biasw", [128, NPW], F32)
        bias_d = sb("a_biasd", [128, 16], F32)
        btok = Tok("bias")
        o0tok = Tok("o0")
        ost = [sb("a_ost%d" % i, [128, 256], BF16) for i in range(4)]
        ostr = Ring(k, ost, with_sem=True, name="ost")
        OC = [sb("a_OC%d" % i, [128, 8 * 128], F32) for i in range(4)]
        octok = Tok("OC")
        selm = sb("a_selm", [128, 128], BF16)
        selmtok = Tok("selm")
        imp = sb("a_imp", [128, 8 * 128], F32)
        imptok = Tok("imp")
        RT = sb("a_RT", [128, 1024], BF16)
        rttok = Tok("RT")
        kcT = sb("a_kcT", [128, 512], BF16)
        vca = sb("a_vca", [128, 4 * 257], BF16)
        cmptok = Tok("cmp")
        VAv = VA[:, :].rearrange("p (t c) -> p t c", c=257)
        VDv = VD[:, :].rearrange("p (t c) -> p t c", c=257)
        vcav = vca[:, :].rearrange("p (t c) -> p t c", c=257)
        o0 = OC[0]
        k.op("pool", lambda e: e.memset(VA[:, :], 1.0), writes=[ktok[3]])
        k.op("pool", lambda e: e.memset(VD[:, :], 1.0), writes=[ktok[4]])
        k.op("pool", lambda e: e.memset(vca[:, :], 1.0), writes=[cmptok])
        k.op("pool", lambda e: e.memset(kcT[:, :], 0.0), writes=[cmptok])
        sring = Ring(k, banks[0:2], name="sbank")
        accb = banks[2:6]
        acct = [Tok("acc%d" % i) for i in range(4)]
        xb = Ring(k, banks[6:8], name="xbank")
        kTall, kTall_tok = T["kTall"]
        vall, vall_tok = T["vall"]
        kTc, kTc_tok = T["kT"]
        vC, vC_tok = T["vC"]
        qTd, qT_tok = T["qT"]
        mix, mix_tok = T["mix"]

        def load_kT(dst, dtok, dsem, idx, npc):
            prs = []
            for pc in range(npc):
                rho, half = pos_chunk_loc(pc)
                prs.append((dst[:, pc * 512:(pc + 1) * 512],
                            kTall[rho * 4096 + idx * 128: rho * 4096 + (idx + 1) * 128, half * 512:(half + 1) * 512]))
            k.dma_group("sp", dsem, prs, reads=[kTall_tok], writes=[dtok])

        def load_kd(slot2, idx):
            k.dma("sp", ksem[2], KD[:, slot2 * 1024:(slot2 + 1) * 1024], kTc[idx * 128:(idx + 1) * 128, :],
                  reads=[kTc_tok], writes=[ktok[2]])

        def load_v(c0, dv, npc, dc=0):
            prs = []
            for pc in range(npc):
                rho, half = pos_chunk_loc(pc)
                prs.append((VAv[:, pc * 4:(pc + 1) * 4, dc:dc + dv],
                            vall[rho * 1024 + half * 512: rho * 1024 + (half + 1) * 512, c0:c0 + dv]
                            .rearrange("(t p) c -> p t c", p=128)))
            k.dma_group("sp", ksem[3], prs, reads=[vall_tok], writes=[ktok[3]])
            k.dma("sp", ksem[4], VDv[:, :, dc:dc + dv], vC[:, c0:c0 + dv].rearrange("(t p) c -> p t c", p=128),
                  reads=[vC_tok], writes=[ktok[4]])

        def load_q(qslot, idx):
            k.dma("sp", ksem[5], QT[:, qslot * 1024:(qslot + 1) * 1024], qTd[idx * 128:(idx + 1) * 128, :],
                  reads=[qT_tok], writes=[ktok[5]])

        def set_bias(hh, with_w):
            k.op("dve", lambda e: e.tensor_scalar(out=bias_h[:, :], in0=posb[:, :], scalar1=slopes[:, hh:hh + 1],
                                                  scalar2=None, op0=ALU.mult), writes=[btok])
            k.op("dve", lambda e: e.tensor_scalar(out=bias_d[:, :], in0=diagb[:, :], scalar1=slopes[:, hh:hh + 1],
                                                  scalar2=None, op0=ALU.mult), writes=[btok])
            if with_w:
                k.op("dve", lambda e: e.tensor_scalar(out=bias_w[:, :], in0=posbw[:, :], scalar1=slopes[:, hh:hh + 1],
                                                      scalar2=None, op0=ALU.mult), writes=[btok])

        def block(q_ap, tiles, ncol):
            started = [False] * 4
            last = {}
            for ti, tl in enumerate(tiles):
                for s in tl["subs"]:
                    last[s] = ti
            for ti, tl in enumerate(tiles):
                b = sring.next()
                sbk = sring.aps[b]
                ex = tl.get("extra", [])
                k.op("pe", lambda e: e.matmul(sbk[:, :], lhsT=tl["lhsT"], rhs=q_ap, start=True, stop=(len(ex) == 0)),
                     reads=tl["rtoks"] + [ktok[5]], writes=[sring.toks[b]], inc=(len(ex) == 0))
                for xi, (xl, xr, xt) in enumerate(ex):
                    k.op("pe", lambda e: e.matmul(sbk[:, :], lhsT=xl, rhs=xr, start=False, stop=(xi == len(ex) - 1)),
                         reads=xt, writes=[sring.toks[b]], inc=(xi == len(ex) - 1))
                pj = ptr.next()
                pt = ptr.aps[pj]
                for s in tl["subs"]:
                    k.op("act", lambda e: e.activation(out=pt[:, s * 128:(s + 1) * 128], in_=sbk[:, s * 128:(s + 1) * 128],
                                                       func=AF.Exp, scale=SCALE, bias=tl["bias"](s)),
                         reads=[sring.toks[b], btok], writes=[ptr.toks[pj]])
                for s in tl["subs"]:
                    k.op("pe", lambda e: e.matmul(accb[s][:, 0:ncol], lhsT=pt[:, s * 128:(s + 1) * 128], rhs=tl["v"],
                                                  start=(not started[s]), stop=(last[s] == ti)),
                         reads=[ptr.toks[pj]] + tl["vtoks"], writes=[acct[s]], inc=(last[s] == ti))
                    started[s] = True

        def std_tiles(slot, KTp, ktp_tok, kd_slot, with_sel, wmode):
            tl = []
            cand = WPAST[slot] if wmode else PAST[slot]
            for j, pc in enumerate(cand):
                for t in range(4):
                    kt = pc * 4 + t
                    d = dict(lhsT=None, v=None, subs=[0, 1, 2, 3], rtoks=[ktp_tok], vtoks=[ktok[3]], extra=[])
                    d["lhsT"] = KTp[:, kt * 128:(kt + 1) * 128]
                    if wmode:
                        c0 = (0 if slot == 0 else 112) + j * 16 + t * 4
                        d["bias"] = (lambda s, c0=c0: bias_w[:, c0 + s:c0 + s + 1])
                        d["extra"].append((ident[:, :], maskc[:, t * 512:(t + 1) * 512], [ctok]))
                    else:
                        c0 = pb_col(slot, j, t, 0)
                        d["bias"] = (lambda s, c0=c0: bias_h[:, c0 + s:c0 + s + 1])
                    if with_sel:
                        d["extra"].append((esel[:, kt * 128:(kt + 1) * 128], RT[:, slot * 512:(slot + 1) * 512], [ctok, rttok]))
                    tl.append(d)
            for t in range(4):
                d = dict(lhsT=KD[:, kd_slot * 1024 + slot * 512 + t * 128: kd_slot * 1024 + slot * 512 + (t + 1) * 128],
                         subs=list(range(t, 4)), rtoks=[ktok[2]], vtoks=[ktok[4]], extra=[])
                d["tdiag"] = t
                d["bias"] = (lambda s, t=t: bias_d[:, t * 4 + s:t * 4 + s + 1])
                d["extra"].append((ident[:, :], maskc[:, (4 + t) * 512:(5 + t) * 512], [ctok]))
                if with_sel:
                    d["extra"].append((ediag[:, (slot * 4 + t) * 128:(slot * 4 + t + 1) * 128],
                                       RT[:, slot * 512:(slot + 1) * 512], [ctok, rttok]))
                tl.append(d)
            return tl

        def set_v(tl, slot, dv):
            for d in tl:
                if "tdiag" in d:
                    d["v"] = VDv[:, slot * 4 + d["tdiag"], 0:dv + 1]
            return tl

        rv = sb("a_rv", [128, 16], F32)
        rvtok = Tok("rv")

        def rinv_of(s, col):
            k.op("dve", lambda e: e.tensor_scalar(out=rv[:, s:s + 1], in0=accb[s][:, col:col + 1], scalar1=1e-30,
                                                  scalar2=None, op0=ALU.max), reads=[acct[s]], writes=[rvtok])
            k.op("dve", lambda e: e.reciprocal(out=rv[:, s:s + 1], in_=rv[:, s:s + 1]), reads=[rvtok], writes=[rvtok])

        for h in range(8):
            load_kT(KT0, ktok[0], ksem[0], 2 * h, 15)
            load_kT(KT1, ktok[1], ksem[1], 2 * h + 1, 15)
            load_kd(0, 2 * h)
            load_kd(1, 2 * h + 1)
            load_v(h * 256, 256, 15)
            load_q(0, 2 * h)
            load_q(1, 2 * h + 1)
            set_bias(h, False)
            for slot in range(2):
                for m in range(2):
                    tl = std_tiles(slot, KT0 if m == 0 else KT1, ktok[m], m, False, False)
                    for d in tl:
                        if "tdiag" not in d:
                            pass
                    ti = 0
                    for j, pc in enumerate(PAST[slot]):
                        for t in range(4):
                            tl[ti]["v"] = VAv[:, pc * 4 + t, 0:257]
                            ti += 1
                    set_v(tl, slot, 256)
                    block(QT[:, m * 1024 + slot * 512: m * 1024 + (slot + 1) * 512], tl, 257)
                    for s in range(4):
                        rinv_of(s, 256)
                        if m == 0:
                            k.op("dve", lambda e: e.tensor_scalar(out=o0[:, s * 256:(s + 1) * 256], in0=accb[s][:, 0:256],
                                                                  scalar1=rv[:, s:s + 1], scalar2=None, op0=ALU.mult),
                                 reads=[acct[s], rvtok], writes=[o0tok])
                        else:
                            k.op("dve", lambda e: e.tensor_tensor(out=rv[:, 8 + s:9 + s], in0=rv[:, s:s + 1], in1=sc[:, 5:6],
                                                                  op=ALU.mult), reads=[rvtok, sctok], writes=[rvtok])
                            osl = o0[:, s * 256:(s + 1) * 256]
                            k.op("dve", lambda e: e.scalar_tensor_tensor(out=osl, in0=accb[s][:, 0:256], scalar=rv[:, 8 + s:9 + s],
                                                                         in1=osl, op0=ALU.mult, op1=ALU.add),
                                 reads=[acct[s], rvtok, o0tok], writes=[o0tok])
                            k.op("act", lambda e: e.activation(out=junk[:, 0:256], in_=osl, func=AF.Square,
                                                               accum_out=rv[:, 12:13]), reads=[o0tok], writes=[jtok, rvtok])
                            k.op("dve", lambda e: e.tensor_scalar(out=rv[:, 13:14], in0=rv[:, 12:13], scalar1=1.0 / 256, scalar2=EPS,
                                                                  op0=ALU.mult, op1=ALU.add), reads=[rvtok], writes=[rvtok])
                            k.op("act", lambda e: e.activation(out=rv[:, 14:15], in_=rv[:, 13:14], func=AF.Sqrt),
                                 reads=[rvtok], writes=[rvtok])
                            k.op("dve", lambda e: e.reciprocal(out=rv[:, 15:16], in_=rv[:, 14:15]), reads=[rvtok], writes=[rvtok])
                            oj = ostr.next()
                            k.op("dve", lambda e: e.scalar_tensor_tensor(out=ostr.aps[oj][:, :], in0=osl, scalar=rv[:, 15:16],
                                                                         in1=dnorm[:, :], op0=ALU.mult, op1=ALU.mult),
                                 reads=[o0tok, rvtok, ctok], writes=[ostr.toks[oj]])
                            r0 = slot * 512 + s * 128
                            k.dma("sp", ostr.ds[oj], mix[r0:r0 + 128, h * 256:(h + 1) * 256], ostr.aps[oj][:, :],
                                  reads=[ostr.toks[oj]], writes=[mix_tok])

        wsa = WStream.__new__(WStream)
        wsa.k = k
        st_t = [sb("a_wst%d" % i, [128, 2048], F32) for i in range(2)]
        bf_t = [sb("a_wbf%d" % i, [128, 2048], BF16) for i in range(2)]
        wsa.stage = Ring(k, st_t, with_sem=True, name="awst")
        wsa.bf = Ring(k, bf_t, name="awbf")
        wsa.queue = []
        wsa.ncast = 0
        w2b = sb("a_w2b", [128, 512], BF16)
        posTb = sb("a_posTb", [128, 64], BF16)
        k.op("dve", lambda e: e.tensor_copy(out=w2b[:, 0:256], in_=w2k[:, :]), reads=[ctok], writes=[cmptok])
        k.op("dve", lambda e: e.tensor_copy(out=w2b[:, 256:512], in_=w2v[:, :]), reads=[ctok], writes=[cmptok])
        k.op("dve", lambda e: e.tensor_copy(out=posTb[:, :], in_=posT[:, :]), reads=[ctok], writes=[cmptok])
        w2bv = w2b[:, :].rearrange("p (m a d) -> p m a d", m=2, a=2)
        KC = KT1
        kctok = ktok[1]
        kcsem = ksem[1]
        k.op("pool", lambda e: e.memset(VAv[:, :, 128:129], 1.0), writes=[ktok[3]])
        k.op("pool", lambda e: e.memset(VDv[:, :, 128:129], 1.0), writes=[ktok[4]])
        xs = sb("a_xs", [128, 512], F32)
        tt_ = sb("a_tt", [128, 512], F32)
        gT = sb("a_gT", [128, 2 * 512], BF16)
        gtok = Tok("gT")
        pw = sb("a_pw", [128, 2], F32)
        k.op("pool", lambda e: e.memset(gT[:, :], 0.0), writes=[gtok])
        GC = 2.0 * 0.7978845608028654

        for kv in range(4):
            for m in range(2):
                prs = []
                idx = 16 + 4 * m + kv
                for pc in range(16):
                    rho, half = pos_chunk_loc(pc)
                    prs.append((KC[:, pc * 512:(pc + 1) * 512],
                                kTall[rho * 4096 + idx * 128: rho * 4096 + (idx + 1) * 128, half * 512:(half + 1) * 512]))
                k.dma_group("sp", kcsem, prs, reads=[kTall_tok], writes=[kctok])
                w1 = X["cmp_w1_k"] if m == 0 else X["cmp_w1_v"]
                pieces = []
                for q4 in range(4):
                    src = w1[q4 * 1024:(q4 + 1) * 1024, :].rearrange("(l d) j -> d l j", d=128)
                    pieces.append((src, (8, 256), q4))
                hb = [xb.next(), xb.next()]

                def consume(i, q4, wb, wtok, hb=hb, m=m):
                    wv = wb[:, :].rearrange("p (l j) -> p l j", l=8)
                    for li in range(8):
                        l = q4 * 8 + li
                        for jc in range(2):
                            k.op("pe", lambda e: e.matmul(xb.aps[hb[jc]][:, 0:511], lhsT=wv[:, li, jc * 128:(jc + 1) * 128],
                                                          rhs=KC[:, l:l + 16 * 510 + 1:16], start=(l == 0), stop=False),
                                 reads=[wtok, kctok], writes=[xb.toks[hb[jc]]], inc=False)
                            k.op("pe", lambda e: e.matmul(xb.aps[hb[jc]][:, 511:512], lhsT=wv[:, li, jc * 128:(jc + 1) * 128],
                                                          rhs=posTb[:, m * 32 + l:m * 32 + l + 1], start=False, stop=(l == 31)),
                                 reads=[wtok, cmptok], writes=[xb.toks[hb[jc]]], inc=(li == 7))
                stream_pieces(wsa, pieces, consume, lookahead=2)
                for jc in range(2):
                    bk = xb.aps[hb[jc]]
                    btk = xb.toks[hb[jc]]
                    k.op("dve", lambda e: e.tensor_copy(out=pw[:, jc:jc + 1], in_=bk[:, 511:512]), reads=[btk], writes=[gtok])
                    k.op("dve", lambda e: e.tensor_scalar(out=xs[:, 0:511], in0=bk[:, 0:511], scalar1=pw[:, jc:jc + 1], scalar2=None,
                                                          op0=ALU.add), reads=[btk, gtok], writes=[gtok])
                    k.op("dve", lambda e: e.tensor_tensor(out=tt_[:, 0:511], in0=xs[:, 0:511], in1=xs[:, 0:511], op=ALU.mult),
                         reads=[gtok], writes=[gtok])
                    k.op("dve", lambda e: e.tensor_scalar(out=tt_[:, 0:511], in0=tt_[:, 0:511], scalar1=0.044715, scalar2=1.0,
                                                          op0=ALU.mult, op1=ALU.add), reads=[gtok], writes=[gtok])
                    k.op("dve", lambda e: e.tensor_tensor(out=tt_[:, 0:511], in0=tt_[:, 0:511], in1=xs[:, 0:511], op=ALU.mult),
                         reads=[gtok], writes=[gtok])
                    k.op("act", lambda e: e.activation(out=tt_[:, 0:511], in_=tt_[:, 0:511], func=AF.Sigmoid, scale=GC),
                         reads=[gtok], writes=[gtok])
                    k.op("dve", lambda e: e.tensor_tensor(out=gT[:, jc * 512:jc * 512 + 511], in0=tt_[:, 0:511], in1=xs[:, 0:511],
                                                          op=ALU.mult), reads=[gtok], writes=[gtok])
                if m == 0:
                    b = xb.next()
                    for jc in range(2):
                        k.op("pe", lambda e: e.matmul(xb.aps[b][:, 0:512], lhsT=w2bv[:, 0, jc, :], rhs=gT[:, jc * 512:(jc + 1) * 512],
                                                      start=(jc == 0), stop=(jc == 1)), reads=[gtok, cmptok], writes=[xb.toks[b]],
                             inc=(jc == 1))
                    k.op("act", lambda e: e.copy(out=kcT[:, 0:511], in_=xb.aps[b][:, 0:511]), reads=[xb.toks[b]], writes=[cmptok])
                else:
                    for ct in range(4):
                        b = xb.next()
                        for jc in range(2):
                            k.op("pe", lambda e: e.matmul(xb.aps[b][:, 0:128], lhsT=gT[:, jc * 512 + ct * 128: jc * 512 + (ct + 1) * 128],
                                                          rhs=w2bv[:, 1, jc, :], start=(jc == 0), stop=(jc == 1)),
                                 reads=[gtok, cmptok], writes=[xb.toks[b]], inc=(jc == 1))
                        k.op("act", lambda e: e.copy(out=vcav[:, ct, 0:128], in_=xb.aps[b][:, 0:128]), reads=[xb.toks[b]], writes=[cmptok])
                        k.op("dve", lambda e: e.tensor_copy(out=vcav[:, ct, 129:257], in_=overlap[:, ct * 128:(ct + 1) * 128]),
                             reads=[ctok], writes=[cmptok])
            load_kT(KT0, ktok[0], ksem[0], 24 + kv, 15)
            load_kT(KT1, ktok[1], ksem[1], 28 + kv, 15)
            load_kd(0, 24 + kv)
            load_kd(1, 28 + kv)
            for g in range(4):
                load_q(g, 16 + kv * 4 + g)
            load_v(2048 + kv * 128, 128, 15, 0)
            load_v(2560 + kv * 128, 128, 15, 129)
            for g in range(4):
                H = kv * 4 + g
                for slot in range(2):
                    tl = []
                    for j in range(4):
                        c0 = (slot * 16 + H) * 16 + j * 4
                        d = dict(lhsT=kcT[:, j * 128:(j + 1) * 128], v=vcav[:, j, :], subs=[0, 1, 2, 3], rtoks=[cmptok],
                                 vtoks=[cmptok], bias=(lambda s, c0=c0: albcmp[:, c0 + s:c0 + s + 1]),
                                 extra=[(ident[:, :], maskcmp[:, (slot * 4 + j) * 512:(slot * 4 + j + 1) * 512], [ctok])])
                        tl.append(d)
                    block(QT[:, g * 1024 + slot * 512: g * 1024 + (slot + 1) * 512], tl, 257)
                    for s in range(4):
                        s8 = slot * 4 + s
                        rinv_of(s, 128)
                        if g == 0:
                            k.op("dve", lambda e: e.tensor_scalar(out=imp[:, s8 * 128:(s8 + 1) * 128], in0=accb[s][:, 129:257],
                                                                  scalar1=rv[:, s:s + 1], scalar2=None, op0=ALU.mult),
                                 reads=[acct[s], rvtok], writes=[imptok])
                        else:
                            k.op("dve", lambda e: e.scalar_tensor_tensor(out=imp[:, s8 * 128:(s8 + 1) * 128], in0=accb[s][:, 129:257],
                                                                         scalar=rv[:, s:s + 1], in1=imp[:, s8 * 128:(s8 + 1) * 128],
                                                                         op0=ALU.mult, op1=ALU.add),
                                 reads=[acct[s], rvtok, imptok], writes=[imptok])
                        gc = s8 * 48 + H * 3
                        k.op("dve", lambda e: e.tensor_tensor(out=rv[:, 8 + s:9 + s], in0=rv[:, s:s + 1], in1=gates[:, gc:gc + 1],
                                                              op=ALU.mult), reads=[rvtok, ctok], writes=[rvtok])
                        k.op("dve", lambda e: e.tensor_scalar(out=OC[g][:, s8 * 128:(s8 + 1) * 128], in0=accb[s][:, 0:128],
                                                              scalar1=rv[:, 8 + s:9 + s], scalar2=None, op0=ALU.mult),
                             reads=[acct[s], rvtok], writes=[octok])
                        if DBG is not None and H == 0:
                            k.op("dve", lambda e: e.tensor_scalar(out=junk[:, 256:384], in0=accb[s][:, 0:128],
                                                                  scalar1=rv[:, s:s + 1], scalar2=None, op0=ALU.mult),
                                 reads=[acct[s], rvtok, DBG["tok"]], writes=[jtok])
                            k.dma("sp", DBG["sem"], DBG["br"][s8 * 128:(s8 + 1) * 128, :], junk[:, 256:384],
                                  reads=[jtok], writes=[DBG["tok"]])
            for s8 in range(8):
                isl = imp[:, s8 * 128:(s8 + 1) * 128]
                A = tkc[:, (s8 * 3 + 0) * 128:(s8 * 3 + 1) * 128]
                Bm = tkc[:, (s8 * 3 + 1) * 128:(s8 * 3 + 2) * 128]
                Fm = tkc[:, (s8 * 3 + 2) * 128:(s8 * 3 + 3) * 128]
                k.op("dve", lambda e: e.tensor_tensor(out=isl, in0=isl, in1=A, op=ALU.mult), reads=[imptok, ctok], writes=[imptok])
                k.op("dve", lambda e: e.tensor_tensor(out=isl, in0=isl, in1=Bm, op=ALU.add), reads=[imptok], writes=[imptok])
                k.op("dve", lambda e: e.tensor_tensor(out=isl, in0=isl, in1=Fm, op=ALU.max), reads=[imptok], writes=[imptok])
                k.op("dve", lambda e: e.max(out=rv[:, 0:8], in_=isl), reads=[imptok], writes=[rvtok])
                k.op("dve", lambda e: e.match_replace(out=junk[:, 0:128], in_to_replace=rv[:, 0:8], in_values=isl, imm_value=-1e9),
                     reads=[imptok, rvtok], writes=[jtok])
                k.op("dve", lambda e: e.max(out=rv[:, 8:16], in_=junk[:, 0:128]), reads=[jtok], writes=[rvtok])
                k.op("dve", lambda e: e.tensor_scalar(out=junk[:, 128:256], in0=isl, scalar1=rv[:, 15:16], scalar2=None, op0=ALU.is_ge),
                     reads=[imptok, rvtok], writes=[jtok])
                k.op("dve", lambda e: e.tensor_scalar(out=selm[:, :], in0=junk[:, 128:256], scalar1=-1.0, scalar2=BIG,
                                                      op0=ALU.add, op1=ALU.mult), reads=[jtok], writes=[selmtok])
                b = xb.next()
                xbf = xb.aps[b].bitcast(BF16)
                k.op("pe", lambda e: e.transpose(out=xbf[:, 0:128], in_=selm[:, :], identity=ident[:, :]),
                     reads=[selmtok, ctok], writes=[xb.toks[b]])
                k.op("act", lambda e: e.copy(out=RT[:, s8 * 128:(s8 + 1) * 128], in_=xbf[:, 0:128]), reads=[xb.toks[b]], writes=[rttok])
            if DBG is not None and kv == 0:
                k.dma_group("sp", DBG["sem"], [(DBG["kcT"][:, :], kcT[:, :]), (DBG["vca"][:, :], vca[:, :]),
                                               (DBG["RT"][:, :], RT[:, :]), (DBG["imp"][:, :], imp[:, :])],
                            reads=[cmptok, rttok, imptok], writes=[DBG["tok"]])
            for g in range(4):
                H = kv * 4 + g
                set_bias(8 + H, True)
                for br in (1, 2):
                    if g == 0:
                        pass
                    for slot in range(2):
                        if br == 1:
                            tl = std_tiles(slot, KT0, ktok[0], 0, True, False)
                            cand = PAST[slot]
                        else:
                            tl = std_tiles(slot, KT1, ktok[1], 1, False, True)
                            cand = WPAST[slot]
                        vc0 = 0 if br == 1 else 128
                        ti = 0
                        for j, pc in enumerate(cand):
                            for t in range(4):
                                tl[ti]["v"] = VAv[:, pc * 4 + t, vc0:vc0 + 129]
                                ti += 1
                        for d in tl:
                            if "tdiag" in d:
                                d["v"] = VDv[:, slot * 4 + d["tdiag"], vc0:vc0 + 129]
                        block(QT[:, g * 1024 + slot * 512: g * 1024 + (slot + 1) * 512], tl, 129)
                        sumc = 128 if br == 1 else 0
                        valc = 0 if br == 1 else 1
                        for s in range(4):
                            s8 = slot * 4 + s
                            rinv_of(s, sumc)
                            gc = s8 * 48 + H * 3 + br
                            k.op("dve", lambda e: e.tensor_tensor(out=rv[:, 8 + s:9 + s], in0=rv[:, s:s + 1], in1=gates[:, gc:gc + 1],
                                                                  op=ALU.mult), reads=[rvtok, ctok], writes=[rvtok])
                            osl = OC[g][:, s8 * 128:(s8 + 1) * 128]
                            if DBG is not None and H == 0:
                                k.op("dve", lambda e: e.tensor_scalar(out=junk[:, 256:384], in0=accb[s][:, valc:valc + 128],
                                                                      scalar1=rv[:, s:s + 1], scalar2=None, op0=ALU.mult),
                                     reads=[acct[s], rvtok, DBG["tok"]], writes=[jtok])
                                k.dma("sp", DBG["sem"], DBG["br"][br * 1024 + s8 * 128: br * 1024 + (s8 + 1) * 128, :], junk[:, 256:384],
                                      reads=[jtok], writes=[DBG["tok"]])
                            k.op("dve", lambda e: e.scalar_tensor_tensor(out=osl, in0=accb[s][:, valc:valc + 128], scalar=rv[:, 8 + s:9 + s],
                                                                         in1=osl, op0=ALU.mult, op1=ALU.add),
                                 reads=[acct[s], rvtok, octok], writes=[octok])
                            if br == 2:
                                oj = ostr.next()
                                k.op("act", lambda e: e.copy(out=ostr.aps[oj][:, 0:128], in_=osl), reads=[octok], writes=[ostr.toks[oj]])
                                r0 = slot * 512 + s * 128
                                k.dma("sp", ostr.ds[oj], mix[r0:r0 + 128, 2048 + H * 128: 2048 + (H + 1) * 128],
                                      ostr.aps[oj][:, 0:128], reads=[ostr.toks[oj]], writes=[mix_tok])
        k.wait_all("sp", [mix_tok])
    barrier(k)
SHARED_IN = [("ffn1_w_gate", [D, DFF], F32), ("ffn1_w_up", [D, DFF], F32), ("ffn1_w_down", [DFF, D], F32),
             ("ffn2_w_gate", [D, DFF], F32), ("ffn2_w_up", [D, DFF], F32), ("ffn2_w_down", [DFF, D], F32),
             ("w_in", [D, N_IN], F32), ("w_out", [D, D], F32),
             ("gainsT", [128, 3 * KT], F32), ("final_norm", [1, D], F32), ("ident", [128, 128], BF16),
             ("gate_bias", [1, 48], F32), ("lam4", [1, 512], F32), ("diff_norm", [1, 256], F32),
             ("cmp_w1_k", [4096, 256], F32), ("cmp_w1_v", [4096, 256], F32),
             ("cmp_w2_k", [256, 128], F32), ("cmp_w2_v", [256, 128], F32),
             ("cmp_posT_k", [128, 32], F32), ("cmp_posT_v", [128, 32], F32),
             ("maskc", [128, 8 * 512], BF16), ("overlap", [128, 4 * 128], BF16), ("esel", [128, 60 * 128], BF16),
             ("diagb", [128, 16], F32), ("slopes", [128, 24], F32)]
PERCORE_IN = [("x", [TOK, D], F32), ("maskcmp", [128, 2 * 4 * 512], BF16), ("albcmp", [128, 2 * 16 * 16], F32),
              ("ediag", [128, 2 * 4 * 128], BF16), ("tkc", [128, 8 * 3 * 128], BF16),
              ("posb", [128, NPB], F32), ("posbw", [128, NPW], F32)]


def build(debug=False):
    nc = bass.Bass("TRN2", target_bir_lowering=False)
    es = contextlib.ExitStack()
    with es:
        k = K(nc, es)
        X = {}
        for nm, shp, dt_ in SHARED_IN + PERCORE_IN:
            X[nm] = nc.dram_tensor(nm, shp, dt_, kind="ExternalInput")
        out = nc.dram_tensor("out", [TOK, D], F32, kind="ExternalOutput")
        T = {}
        for nm, shp, dt_ in (("h1", [TOK, D], F32), ("h2", [TOK, D], F32), ("qT", [32 * 128, TOK], BF16),
                             ("kT", [32 * 128, TOK], BF16), ("vC", [TOK, 3072], BF16),
                             ("kTall", [8 * 32 * 128, TOK], BF16), ("vall", [8 * TOK, 3072], BF16),
                             ("gates", [TOK, 48], F32), ("mix", [TOK, D], BF16)):
            T[nm] = (nc.dram_tensor(nm, shp, dt_), Tok(nm))
        x_tok, out_tok = Tok("x"), Tok("out")
        banks = [es.enter_context(nc.psum_tensor("bank%d" % i, [128, 512], F32)) for i in range(8)]

        with contextlib.ExitStack() as es1:
            C = alloc_dense(k, nc, es1, banks)
            ident = es1.enter_context(nc.sbuf_tensor("ident_sb", [128, 128], BF16))
            gsb = es1.enter_context(nc.sbuf_tensor("gains_sb", [128, 3 * KT], F32))
            gbias = es1.enter_context(nc.sbuf_tensor("gbias_sb", [128, 48], F32))
            C["ident"] = ident
            C["gbias"] = gbias
            cs = k.dsem()
            ctok = Tok("consts")
            k.dma_group("sp", cs, [(ident[:, :], X["ident"][:, :]), (gsb[:, :], X["gainsT"][:, :]),
                                   (gbias[:, :], X["gate_bias"][0:1, :].broadcast_to([128, 48]))], writes=[ctok])
            for e in ("pe", "act", "dve", "pool"):
                k.wait_all(e, [ctok])
            for p in range(2):
                ffn_pass(k, nc, C, C["ws"], X["x"], x_tok, T["h1"][0], T["h1"][1], gsb[:, 0:KT],
                         X["ffn1_w_gate"], X["ffn1_w_up"], X["ffn1_w_down"], p * TP)
            for p in range(2):
                proj_pass(k, nc, C, T["h1"][0], T["h1"][1], gsb[:, KT:2 * KT], X["w_in"], p * TP, T)
            barrier(k)

        for src, dst in (("kT", "kTall"), ("vC", "vall")):
            cc = k.dsem()
            pe_ = k.engs["pool"]
            k._wait(pe_, k._deps([T[src][1]], [T[dst][1]]))
            pe_.e.collective_compute("AllGather", ALU.bypass, replica_groups=[list(range(NCORES))],
                                     ins=[T[src][0].ap().opt()], outs=[T[dst][0].ap().opt()]).then_inc(cc.sem, 1)
            cc.n += 1
            T[dst][1].w = (cc.key, cc.n)
            T[dst][1].r = []
            T[src][1].r.append((cc.key, cc.n))

        DBG = None
        if debug:
            DBG = {nm: nc.dram_tensor("dbg_" + nm, shp, dt_, kind="ExternalOutput") for nm, shp, dt_ in
                   (("br", [3 * 1024, 128], F32), ("kcT", [128, 512], BF16), ("vca", [128, 4 * 257], BF16),
                    ("RT", [128, 1024], BF16), ("imp", [128, 1024], F32))}
            DBG["sem"] = k.dsem()
            DBG["tok"] = Tok("dbgA")
        attention_phase(k, nc, banks, T, X, DBG)

        with contextlib.ExitStack() as es2:
            C = alloc_dense(k, nc, es2, banks)
            ident = es2.enter_context(nc.sbuf_tensor("ident_sb2", [128, 128], BF16))
            gsb = es2.enter_context(nc.sbuf_tensor("gains_sb2", [128, 3 * KT], F32))
            fg = es2.enter_context(nc.sbuf_tensor("fgain_sb", [128, D], F32))
            C["ident"] = ident
            cs = k.dsem()
            ctok = Tok("consts2")
            k.dma_group("sp", cs, [(ident[:, :], X["ident"][:, :]), (gsb[:, :], X["gainsT"][:, :]),
                                   (fg[:, :], X["final_norm"][0:1, :].broadcast_to([128, D]))], writes=[ctok])
            for e in ("pe", "act", "dve", "pool"):
                k.wait_all(e, [ctok])
            for p in range(2):
                wout_pass(k, nc, C, T["h1"][0], T["h1"][1], T["mix"][0], T["mix"][1], X["w_out"],
                          T["h2"][0], T["h2"][1], p * TP)
            for p in range(2):
                ffn_pass(k, nc, C, C["ws"], T["h2"][0], T["h2"][1], out, out_tok, gsb[:, 2 * KT:3 * KT],
                         X["ffn2_w_gate"], X["ffn2_w_up"], X["ffn2_w_down"], p * TP, final_gain=fg)
            k.wait_all("sp", [out_tok])
            barrier(k)
        if debug:
            dsm = k.dsem()
            dtk = Tok("dbg")
            for nm in ("h1", "h2", "qT", "kT", "vC", "gates", "mix"):
                th = T[nm][0]
                dd = nc.dram_tensor("dbg_" + nm, list(th.shape), th.dtype, kind="ExternalOutput")
                k.dma("sp", dsm, dd[:, :], th[:, :], reads=[T[nm][1]], writes=[dtk])
            k.wait_all("sp", [dtk])
    return nc


def _bf(a):
    return np.ascontiguousarray(a.astype(np.float32)).astype(ml_dtypes.bfloat16)


def const_tables():
    p = np.arange(128)[:, None].astype(np.float64)
    f = np.arange(512)[None, :].astype(np.float64)
    t = {}
    mc = np.zeros((128, 8, 512), np.float32)
    for i in range(8):
        dist = f - (128 * (i - 4) + p)
        mc[:, i, :] = np.where((dist >= 0) & (dist < 512), 0.0, -BIG)
    t["maskc"] = _bf(mc.reshape(128, -1))
    ov = np.zeros((128, 4, 128), np.float32)
    n = np.arange(128)[None, :]
    for j in range(4):
        c = 128 * j + np.arange(128)[:, None]
        ov[:, j, :] = ((16 * c <= 64 * n + 63) & (16 * c + 31 >= 64 * n) & (c <= 510)).astype(np.float32)
    t["overlap"] = _bf(ov.reshape(128, -1))
    es_ = np.zeros((128, 60, 128), np.float32)
    b = np.arange(128)[:, None]
    pk = np.arange(128)[None, :]
    for kt in range(60):
        es_[:, kt, :] = (b == 2 * kt + pk // 64).astype(np.float32)
    t["esel"] = _bf(es_.reshape(128, -1))
    db = np.zeros((128, 16), np.float32)
    for tt in range(4):
        for s in range(4):
            db[:, tt * 4 + s] = 128 * (tt - s) + np.arange(128) - 64
    t["diagb"] = db
    sl = np.concatenate([2.0 ** (-np.arange(1, 9, dtype=np.float64)), 2.0 ** (-0.5 * np.arange(1, 17, dtype=np.float64))])
    t["slopes"] = np.ascontiguousarray(np.broadcast_to(sl[None, :], (128, 24)).astype(np.float32))
    t["ident"] = np.eye(128, dtype=ml_dtypes.bfloat16)
    return t, sl


def core_tables(r, sl):
    t = {}
    p = np.arange(128).astype(np.float64)
    cis = (r, 15 - r)
    posb = np.zeros((128, NPB), np.float32)
    posbw = np.zeros((128, NPW), np.float32)
    for slot in range(2):
        ci = cis[slot]
        for j, pc in enumerate(PAST[slot]):
            for tt in range(4):
                for s in range(4):
                    v = (512 * pc + 128 * tt + p) - (512 * ci + 128 * s + 64) if pc < ci else np.full(128, -3e6)
                    posb[:, pb_col(slot, j, tt, s)] = v
        for j, pc in enumerate(WPAST[slot]):
            for tt in range(4):
                for s in range(4):
                    v = (512 * pc + 128 * tt + p) - (512 * ci + 128 * s + 64) if pc == ci - 1 else np.full(128, -3e6)
                    posbw[:, (0 if slot == 0 else 112) + j * 16 + tt * 4 + s] = v
    t["posb"], t["posbw"] = posb, posbw
    f = np.arange(512)[None, :]
    mcmp = np.zeros((128, 2, 4, 512), np.float32)
    alb = np.zeros((128, 2, 16, 4, 4), np.float32)
    ed = np.zeros((128, 2, 4, 128), np.float32)
    tk = np.zeros((128, 8, 3, 128), np.float32)
    b = np.arange(128)[:, None]
    pk = np.arange(128)[None, :]
    n = np.arange(128)[None, :]
    for slot in range(2):
        ci = cis[slot]
        for j in range(4):
            c = (128 * j + np.arange(128))[:, None]
            mcmp[:, slot, j, :] = np.where((c <= 510) & (16 * c + 31 <= 512 * ci + f), 0.0, -BIG)
            for H in range(16):
                for s in range(4):
                    v = sl[8 + H] * (16 * c[:, 0] + 31 - (512 * ci + 128 * s + 64))
                    alb[:, slot, H, j, s] = np.clip(v, -1e4, 80.0)
        for tt in range(4):
            ed[:, slot, tt, :] = (b == 2 * (4 * ci + tt) + pk // 64).astype(np.float32)
        for s in range(4):
            tq = (512 * ci + 128 * s + np.arange(128))[:, None]
            cur = tq // 64
            valid = (n <= cur)
            forced = (n == 0) | (n == cur) | (n == cur - 1)
            tk[:, slot * 4 + s, 0, :] = valid
            tk[:, slot * 4 + s, 1, :] = -(1.0 - valid)
            tk[:, slot * 4 + s, 2, :] = np.where(forced, 1e4, -2.0)
    t["maskcmp"] = _bf(mcmp.reshape(128, -1))
    t["albcmp"] = np.ascontiguousarray(alb.reshape(128, -1))
    t["ediag"] = _bf(ed.reshape(128, -1))
    t["tkc"] = _bf(tk.reshape(128, -1))
    return t


def host_layout(inputs):
    x = np.ascontiguousarray(inputs["x"][0])
    gainsT = np.concatenate([inputs[n][0].reshape(KT, 128).T for n in ("ffn1_norm", "mix_norm", "ffn2_norm")], 1)
    shared, sl = const_tables()
    shared["gainsT"] = np.ascontiguousarray(gainsT.astype(np.float32))
    shared["final_norm"] = np.ascontiguousarray(inputs["final_norm"].reshape(1, D))
    shared["gate_bias"] = np.ascontiguousarray(inputs["gate_bias"].reshape(1, 48))
    shared["lam4"] = np.ascontiguousarray(np.concatenate(
        [inputs[n].reshape(-1) for n in ("lambda_q1", "lambda_k1", "lambda_q2", "lambda_k2")]).reshape(1, 512))
    shared["diff_norm"] = np.ascontiguousarray(inputs["diff_norm"].reshape(1, 256))
    shared["cmp_posT_k"] = np.ascontiguousarray(inputs["cmp_pos_k"][0].T)
    shared["cmp_posT_v"] = np.ascontiguousarray(inputs["cmp_pos_v"][0].T)
    for nm in ("ffn1_w_gate", "ffn1_w_up", "ffn1_w_down", "ffn2_w_gate", "ffn2_w_up", "ffn2_w_down", "w_in", "w_out",
               "cmp_w1_k", "cmp_w1_v", "cmp_w2_k", "cmp_w2_v"):
        shared[nm] = np.ascontiguousarray(inputs[nm][0])
    in_maps = []
    for r in range(NCORES):
        m = dict(shared)
        m.update(core_tables(r, sl))
        m["x"] = np.ascontiguousarray(np.concatenate([x[512 * r:512 * (r + 1)], x[512 * (15 - r):512 * (16 - r)]], 0))
        in_maps.append(m)
    return in_maps


def gather_out(res):
    out = np.zeros((S, D), np.float32)
    for r in range(NCORES):
        o = res[r]["out"]
        out[512 * r:512 * (r + 1)] = o[:512]
        out[512 * (15 - r):512 * (16 - r)] = o[512:]
    return out[None]


DEBUG = False
_LAST = None


def kernel(**inputs):
    global _LAST
    nc = build(debug=DEBUG)
    in_maps = host_layout(inputs)
    res = run_bass_kernel_spmd(nc, in_maps, core_ids=list(range(NCORES)))
    if DEBUG:
        _LAST = res.results
    return gather_out(res.results)
```
